# Optimizing a Trainium2 kernel written in Bass

```python
import math
import jax, jax.numpy as jnp
from jax import lax
import numpy as np

D_MODEL = 1024
BATCH = 8
SEQ = 8192
DEPTH = 1

ATT_HEADS = 8
ATT_QK_DIM = 64
ATT_V_DIM = 2 * ATT_QK_DIM
ATT_QK_WIDTH = ATT_HEADS * 2 * ATT_QK_DIM
ATT_WIDTH = ATT_HEADS * ATT_V_DIM
Q_BLOCK = 128
REL_BUCKETS = 32
REL_MAX_DIST = 128
SSM_INNER = 2 * D_MODEL
SSM_HEAD_DIM = 64
SSM_HEADS = SSM_INNER // SSM_HEAD_DIM
SSM_GROUPS = 8
SSM_HEADS_PER_GROUP = SSM_HEADS // SSM_GROUPS
SSM_STATE = 128
SSM_CONV = 4
SSM_CHUNK = 256
SSM_CONV_DIM = SSM_INNER + 2 * SSM_GROUPS * SSM_STATE
N_BRANCH = 2
EPS = 1e-6
SPLITS = (ATT_QK_WIDTH, ATT_QK_WIDTH, ATT_WIDTH, ATT_WIDTH,
          SSM_INNER, SSM_CONV_DIM, SSM_HEADS, N_BRANCH * D_MODEL)
D_IN_PROJ = sum(SPLITS)
SPLIT_IDX = [int(v) for v in np.cumsum(SPLITS)[:-1]]

kernel_name = "hybrid_diffattn_mamba2_gated_parallel"


def lambda_init_fn(layer_idx):
    return 0.8 - 0.6 * math.exp(-0.3 * layer_idx)


def rmsnorm(x, g):
    xf = x.astype(jnp.float32)
    y = xf * lax.rsqrt(jnp.mean(xf * xf, axis=-1, keepdims=True) + EPS)
    return (y * g.astype(jnp.float32)).astype(x.dtype)


def group_rmsnorm(y, g, groups):
    shape = y.shape
    yg = y.reshape(*shape[:-1], groups, shape[-1] // groups).astype(jnp.float32)
    yg = yg * lax.rsqrt(jnp.mean(yg * yg, axis=-1, keepdims=True) + EPS)
    return (yg.reshape(shape) * g.astype(jnp.float32)).astype(y.dtype)


def t5_bucket(n):
    max_exact = REL_BUCKETS // 2
    nf = jnp.maximum(n, 1).astype(jnp.float32)
    large = max_exact + (jnp.log(nf / max_exact) / math.log(REL_MAX_DIST / max_exact)
                         * (REL_BUCKETS - max_exact)).astype(jnp.int32)
    large = jnp.minimum(large, REL_BUCKETS - 1)
    return jnp.where(n < max_exact, n, large)


def diff_attention(q, k, v, rel_bias, lam):
    b, s, h = q.shape[0], q.shape[1], q.shape[2]
    nblk = s // Q_BLOCK
    qb = q.reshape(b, nblk, Q_BLOCK, h, 2, ATT_QK_DIM).swapaxes(0, 1)
    k_pos = jnp.arange(s)
    scale = ATT_QK_DIM ** -0.5

    def block(args):
        i, q_i = args
        q_pos = i * Q_BLOCK + jnp.arange(Q_BLOCK)
        dist = q_pos[:, None] - k_pos[None, :]
        causal = dist >= 0
        bias = rel_bias[t5_bucket(jnp.maximum(dist, 0))]
        bias = jnp.transpose(bias, (2, 0, 1)).astype(jnp.float32)
        logits = jnp.einsum('bqhcd,bkhcd->bhcqk', q_i, k).astype(jnp.float32) * scale
        logits = logits + bias[None, :, None]
        logits = jnp.where(causal, logits, -jnp.inf)
        p = jax.nn.softmax(logits, axis=-1)
        a = p[:, :, 0] - lam * p[:, :, 1]
        return jnp.einsum('bhqk,bkhe->bqhe', a.astype(v.dtype), v)

    out = lax.map(block, (jnp.arange(nblk), qb))
    return out.swapaxes(0, 1).reshape(b, s, h, ATT_V_DIM)


def causal_depthwise_conv(u, w, bias):
    y = lax.conv_general_dilated(u, w[:, None, :].astype(u.dtype), window_strides=(1,),
                                 padding=[(SSM_CONV - 1, 0)],
                                 dimension_numbers=('NWC', 'WIO', 'NWC'),
                                 feature_group_count=u.shape[-1])
    return y + bias


def ssd_chunked_scan(xs, dt, A, Bm, Cm):
    b, s = xs.shape[0], xs.shape[1]
    pad = (-s) % SSM_CHUNK
    sp = s + pad
    nc = sp // SSM_CHUNK
    G, HG, P, N, L = SSM_GROUPS, SSM_HEADS_PER_GROUP, SSM_HEAD_DIM, SSM_STATE, SSM_CHUNK
    X = xs * dt[..., None]
    Adt = dt.astype(jnp.float32) * A.astype(jnp.float32)
    X = jnp.pad(X, ((0, 0), (0, pad), (0, 0), (0, 0)))
    Adt = jnp.pad(Adt, ((0, 0), (0, pad), (0, 0)))
    Bp = jnp.pad(Bm, ((0, 0), (0, pad), (0, 0), (0, 0)))
    Cp = jnp.pad(Cm, ((0, 0), (0, pad), (0, 0), (0, 0)))
    Xc = X.reshape(b, nc, L, G, HG, P).swapaxes(0, 1)
    Ac = Adt.reshape(b, nc, L, SSM_HEADS).swapaxes(0, 1)
    Bc = Bp.reshape(b, nc, L, G, N).swapaxes(0, 1)
    Cc = Cp.reshape(b, nc, L, G, N).swapaxes(0, 1)
    idx = jnp.arange(L)
    tril = idx[:, None] >= idx[None, :]

    def step(state, inp):
        Xk, Ak, Bk, Ck = inp
        a_cs = jnp.cumsum(Ak, axis=1)
        a_cs = a_cs.transpose(0, 2, 1).reshape(b, G, HG, L)
        seg = a_cs[..., :, None] - a_cs[..., None, :]
        Lmat = jnp.exp(jnp.where(tril, seg, -jnp.inf))
        cb = jnp.einsum('blgn,bsgn->bgls', Ck, Bk)
        y_diag = jnp.einsum('bgls,bghls,bsghp->blghp', cb, Lmat, Xk)
        y_off = jnp.einsum('blgn,bghpn,bghl->blghp', Ck, state, jnp.exp(a_cs))
        decay_to_end = jnp.exp(a_cs[..., -1:] - a_cs)
        new_state = state * jnp.exp(a_cs[..., -1])[..., None, None] + \
            jnp.einsum('bsgn,bghs,bsghp->bghpn', Bk, decay_to_end, Xk)
        return new_state, (y_diag + y_off).astype(jnp.float32)

    state0 = jnp.zeros((b, G, HG, P, N), jnp.float32)
    _, y = lax.scan(step, state0, (Xc, Ac, Bc, Cc))
    y = y.swapaxes(0, 1).reshape(b, sp, SSM_HEADS, P)
    return y[:, :s]


def setup_inputs(seed: int = 0) -> dict:
    key = jax.random.key(seed)
    ks = jax.random.split(key, 20)
    f32 = jnp.float32
    x = jax.random.normal(ks[0], (BATCH, SEQ, D_MODEL), f32)
    g_pre = 1.0 + 0.05 * jax.random.normal(ks[1], (DEPTH, D_MODEL), f32)
    w_in = jax.random.normal(ks[2], (DEPTH, D_MODEL, D_IN_PROJ), f32) * D_MODEL ** -0.5
    att_lambda_q1 = 0.1 * jax.random.normal(ks[3], (DEPTH, ATT_QK_DIM), f32)
    att_lambda_k1 = 0.1 * jax.random.normal(ks[4], (DEPTH, ATT_QK_DIM), f32)
    att_lambda_q2 = 0.1 * jax.random.normal(ks[5], (DEPTH, ATT_QK_DIM), f32)
    att_lambda_k2 = 0.1 * jax.random.normal(ks[6], (DEPTH, ATT_QK_DIM), f32)
    att_subln_g = 1.0 + 0.05 * jax.random.normal(ks[7], (DEPTH, ATT_V_DIM), f32)
    rel_bias = 0.5 * jax.random.normal(ks[8], (REL_BUCKETS, ATT_HEADS), f32)
    conv_w = jax.random.normal(ks[9], (DEPTH, SSM_CONV, SSM_CONV_DIM), f32) * SSM_CONV ** -0.5
    conv_b = 0.02 * jax.random.normal(ks[10], (DEPTH, SSM_CONV_DIM), f32)
    u = jax.random.uniform(ks[11], (DEPTH, SSM_HEADS), f32)
    dt0 = jnp.exp(u * (math.log(0.1) - math.log(0.001)) + math.log(0.001))
    dt_bias = dt0 + jnp.log(-jnp.expm1(-dt0))
    a_log = jnp.log(jax.random.uniform(ks[12], (DEPTH, SSM_HEADS), f32, 1.0, 16.0))
    d_skip = 1.0 + 0.1 * jax.random.normal(ks[13], (DEPTH, SSM_HEADS), f32)
    ssm_norm_g = 1.0 + 0.05 * jax.random.normal(ks[14], (DEPTH, SSM_INNER), f32)
    w_att_proj = jax.random.normal(ks[15], (DEPTH, ATT_WIDTH, D_MODEL), f32) * ATT_WIDTH ** -0.5
    w_ssm_proj = jax.random.normal(ks[16], (DEPTH, SSM_INNER, D_MODEL), f32) * SSM_INNER ** -0.5
    w_out = jax.random.normal(ks[17], (DEPTH, D_MODEL, D_MODEL), f32) * D_MODEL ** -0.5
    g_post = 1.0 + 0.05 * jax.random.normal(ks[18], (DEPTH, D_MODEL), f32)
    return {"x": x, "g_pre": g_pre, "w_in": w_in,
            "att_lambda_q1": att_lambda_q1, "att_lambda_k1": att_lambda_k1,
            "att_lambda_q2": att_lambda_q2, "att_lambda_k2": att_lambda_k2,
            "att_subln_g": att_subln_g, "rel_bias": rel_bias,
            "conv_w": conv_w, "conv_b": conv_b, "dt_bias": dt_bias, "a_log": a_log,
            "d_skip": d_skip, "ssm_norm_g": ssm_norm_g,
            "w_att_proj": w_att_proj, "w_ssm_proj": w_ssm_proj, "w_out": w_out,
            "g_post": g_post}


def reference(x, g_pre, w_in, att_lambda_q1, att_lambda_k1, att_lambda_q2, att_lambda_k2,
              att_subln_g, rel_bias, conv_w, conv_b, dt_bias, a_log, d_skip, ssm_norm_g,
              w_att_proj, w_ssm_proj, w_out, g_post):
    b, s, _ = x.shape
    for layer in range(DEPTH):
        lambda_init = lambda_init_fn(layer)
        h = rmsnorm(x, g_pre[layer])
        proj = h @ w_in[layer]
        q, k, v, g_att, z, xbc, dt_raw, merge_logits = jnp.split(proj, SPLIT_IDX, axis=-1)

        q = q.reshape(b, s, ATT_HEADS, 2, ATT_QK_DIM)
        k = k.reshape(b, s, ATT_HEADS, 2, ATT_QK_DIM)
        v = v.reshape(b, s, ATT_HEADS, ATT_V_DIM)
        lam = (jnp.exp(jnp.sum(att_lambda_q1[layer].astype(jnp.float32) * att_lambda_k1[layer].astype(jnp.float32)))
               - jnp.exp(jnp.sum(att_lambda_q2[layer].astype(jnp.float32) * att_lambda_k2[layer].astype(jnp.float32)))
               + lambda_init)
        o = diff_attention(q, k, v, rel_bias, lam)
        o = rmsnorm(o, att_subln_g[layer]) * (1.0 - lambda_init)
        o = o.reshape(b, s, ATT_WIDTH) * jax.nn.silu(g_att)
        y_att = o @ w_att_proj[layer]

        xbc = jax.nn.silu(causal_depthwise_conv(xbc, conv_w[layer], conv_b[layer]))
        xs, Bm, Cm = jnp.split(xbc, [SSM_INNER, SSM_INNER + SSM_GROUPS * SSM_STATE], axis=-1)
        xs = xs.reshape(b, s, SSM_HEADS, SSM_HEAD_DIM)
        Bm = Bm.reshape(b, s, SSM_GROUPS, SSM_STATE)
        Cm = Cm.reshape(b, s, SSM_GROUPS, SSM_STATE)
        dt = jax.nn.softplus((dt_raw + dt_bias[layer]).astype(jnp.float32))
        A = -jnp.exp(a_log[layer].astype(jnp.float32))
        y = ssd_chunked_scan(xs, dt.astype(xs.dtype), A, Bm, Cm)
        y = y + xs.astype(jnp.float32) * d_skip[layer].astype(jnp.float32)[:, None]
        y = y.reshape(b, s, SSM_INNER).astype(x.dtype) * jax.nn.silu(z)
        y = group_rmsnorm(y, ssm_norm_g[layer], SSM_GROUPS)
        y_ssm = y @ w_ssm_proj[layer]

        gate_att, gate_ssm = jnp.split(merge_logits, 2, axis=-1)
        mixed = jax.nn.sigmoid(gate_att) * y_att + jax.nn.sigmoid(gate_ssm) * y_ssm
        out = mixed @ w_out[layer]
        x = x + rmsnorm(out, g_post[layer])
    return x
```

```python
import math
import os
CDBG = int(os.environ.get('CDBG', '9'))
STRICT = int(os.environ.get('STRICT', '0'))
E15 = int(os.environ.get('E15', '9'))
DDBG = int(os.environ.get('DDBG', '9'))
DPRE = int(os.environ.get('DPRE', '9'))
DDBG2 = int(os.environ.get('DDBG2', '9'))
DDBG3 = int(os.environ.get('DDBG3', '3'))
DDBG4 = int(os.environ.get('DDBG4', '0'))
from contextlib import ExitStack
import numpy as np
import concourse.bass as bass
import concourse.mybir as mybir
from concourse.bass_utils import run_bass_kernel_spmd

F32 = mybir.dt.float32
BF16 = mybir.dt.bfloat16
AF = mybir.ActivationFunctionType
ALU = mybir.AluOpType

D = 1024
NH = 8
EPS = 1e-6
LAMBDA_INIT = 0.8 - 0.6 * math.exp(0.0)
NEG = -30000.0
FW = 1152
FOFF = 511
C_ID, C_ONE, C_TRI0, C_TRI1, C_SUT, C_MK0, C_MK1, C_END = 0, 128, 256, 512, 768, 896, 1152, 1408


def t5_bucket_np(n):
    n = np.asarray(n)
    nf = np.maximum(n, 1).astype(np.float32)
    large = 16 + (np.log(nf / np.float32(16)) / np.float32(math.log(128 / 16)) * np.float32(16)).astype(np.int32)
    large = np.minimum(large, 31)
    return np.where(n < 16, n, large)


class Ctr:
    def __init__(self, h):
        self.h = h
        self.v = 0
        self.seals = []
        self.shared = False


class Res:
    def __init__(self, t):
        self.t = t
        self.w = {}
        self.r = {}
        self.ds = None

    def __getitem__(self, idx):
        return self.t[idx]


class View:
    def __init__(self, t, pre):
        self.t = t
        self.pre = tuple(pre)

    def __getitem__(self, idx):
        if not isinstance(idx, tuple):
            idx = (idx,)
        return self.t[(idx[0],) + self.pre + tuple(idx[1:])]


class KB:
    def __init__(self, nc):
        self.nc = nc
        self.E = dict(pe=nc.tensor, act=nc.scalar, dve=nc.vector, pool=nc.gpsimd, sp=nc.sync)
        self.known = {e: {} for e in self.E}
        self.root = ExitStack()
        self.esem = {}
        self.semnum = int(os.environ.get('SEMBASE', '0'))
        for e in ('pe', 'act', 'dve'):
            self.esem[e] = Ctr(self.root.enter_context(self._sem("es_" + e)))
        for i in range(int(os.environ.get('SEMPAD', '0'))):
            self.root.enter_context(nc.semaphore("pad%d" % i))
        self.dpool = []
        self.all_d = []
        self.ps = None
        self.phase_res = []
        self.n = 0
        self.ninst = 0

    def _sem(self, name):
        if self.semnum > 0:
            self.semnum += 1
            return self.nc.semaphore(name, num=self.semnum - 1)
        return self.nc.semaphore(name)

    def _name(self, p):
        self.n += 1
        return f"{p}{self.n}"

    def begin(self):
        self.ps = ExitStack()
        self.phase_res = []

    def sb(self, shape, dt, root=False):
        st = self.root if root else self.ps
        r = Res(st.enter_context(self.nc.sbuf_tensor(self._name("sb"), list(shape), dt)))
        self.phase_res.append(r)
        return r

    def share(self, tiles):
        c = None
        for t in tiles:
            if t.ds is not None:
                c = t.ds
        if c is None:
            c = self._dsem(tiles[0])
        c.shared = True
        for t in tiles:
            t.ds = c

    def seal(self, tiles):
        c = tiles[0].ds
        c.seals.append(c.v)
        for t in tiles:
            t.w = {c: c.v}

    def psum(self, shape, dt):
        r = Res(self.ps.enter_context(self.nc.psum_tensor(self._name("ps"), list(shape), dt)))
        self.phase_res.append(r)
        return r

    def psum_views(self, n, shape, dt):
        t = self.ps.enter_context(self.nc.psum_tensor(self._name("ps"), [shape[0], n] + list(shape[1:]), dt))
        out = [Res(View(t, (k,))) for k in range(n)]
        self.phase_res.extend(out)
        return out

    def _dsem(self, res):
        if res.ds is None:
            if self.dpool:
                res.ds = self.dpool.pop()
            else:
                res.ds = Ctr(self.root.enter_context(self._sem(self._name("ds"))))
                self.all_d.append(res.ds)
        return res.ds

    def I(self, eng, fn, r=(), w=(), dma=None):
        waits = {}
        mine = self.esem.get(eng) if dma is None else None

        def need(evs, skip_same):
            for c, v in evs.items():
                if skip_same and c is mine:
                    continue
                if c.shared:
                    last = c.seals[-1] if c.seals else 0
                    if v > last:
                        if dma is not None and dma.ds is c:
                            continue
                        raise AssertionError("wait on unsealed shared semaphore")
                    v = min(x for x in c.seals if x >= v)
                if waits.get(c, 0) < v:
                    waits[c] = v
        for t in r:
            need(t.w, False)
        for t in w:
            need(t.w, eng == 'pe' or not STRICT)
            need(t.r, eng == 'pe' or not STRICT)
        kn = self.known[eng]
        E = self.E[eng]
        for c, v in waits.items():
            if kn.get(c, 0) >= v:
                continue
            E.wait_ge(c.h, v)
            kn[c] = v
            self.ninst += 1
        ins = fn(E)
        self.ninst += 1
        if dma is not None:
            c = self._dsem(dma)
            c.v += 16
            ins.then_inc(c.h, 16)
        else:
            c = self.esem[eng]
            c.v += 1
            ins.then_inc(c.h, 1)
        v = c.v
        for t in r:
            t.r[c] = v
        for t in w:
            t.w = {c: v}
            t.r = {}
        return ins

    def barrier(self):
        sems = list(self.esem.values()) + self.all_d
        for eng, E in self.E.items():
            kn = self.known[eng]
            for c in sems:
                if c.v > 0 and kn.get(c, 0) < c.v:
                    E.wait_ge(c.h, c.v)
                    kn[c] = c.v
                    self.ninst += 1

    def end(self):
        self.barrier()
        seen = set(id(c) for c in self.dpool)
        for r in self.phase_res:
            if r.ds is not None:
                if id(r.ds) not in seen:
                    self.dpool.append(r.ds)
                    seen.add(id(r.ds))
                r.ds.shared = False
                r.ds.seals = []
                r.ds = None
        self.ps.close()
        self.ps = None


def bc_ap(ap, dims):
    return bass.AP(tensor=ap.tensor, offset=ap.offset, ap=[ap.ap[0]] + [list(d) for d in dims])


def build_program(S, dbg=False, upto=-1):
    NT = S // 128
    NQ = S // 512
    NCH = S // 256
    nc = bass.Bass("TRN2", target_bir_lowering=False)

    def din(name, shape, dt=F32):
        return nc.dram_tensor(name, list(shape), dt, kind="ExternalInput").ap()

    def dscr(name, shape, dt):
        return nc.dram_tensor(name, list(shape), dt, kind="ExternalOutput" if dbg else "Internal").ap()

    x_ap = din("x", [S, D])
    w_in = din("w_in", [D, 12320])
    cst = din("cst", [128, C_END])
    gpre_l = din("gpre_l", [128, 8])
    rbT = din("rbT", [8, 32])
    lamv = din("lamv", [4, 64])
    subln_l = din("subln_l", [128, 1])
    convw_l = din("convw_l", [128, 32, 4])
    convb_l = din("convb_l", [128, 32])
    dtb = din("dtb", [1, 32])
    alog = din("alog", [1, 32])
    dsk = din("dsk", [1, 32])
    gn_l = din("gn_l", [128, 16])
    w_att = din("w_att", [1024, 1024])
    w_ssm = din("w_ssm", [2048, 1024])
    w_out = din("w_out", [1024, 1024])
    gpost = din("gpost", [1, 1024])
    y_ap = nc.dram_tensor("y", [S, D], F32, kind="ExternalOutput").ap()

    hT = dscr("hT", [D, S], BF16)
    qT = dscr("qT", [1024, S], BF16)
    kT = dscr("kT", [1024, S], BF16)
    v_tm = dscr("v_tm", [S, 1024], BF16)
    gT = dscr("gT", [1024, S], BF16)
    szT = dscr("szT", [2048, S], BF16)
    xbcT = dscr("xbcT", [4096, S], BF16)
    dt_tm = dscr("dt_tm", [S, 32], F32)
    a_tm = dscr("a_tm", [S, 32], F32)
    gaT = dscr("gaT", [1024, S], BF16)
    gsT = dscr("gsT", [1024, S], BF16)
    ogT = dscr("ogT", [1024, S], BF16)
    xs_tm = dscr("xs_tm", [S, 2048], BF16)
    b_tm = dscr("b_tm", [S, 1024], BF16)
    bT = dscr("bT", [1024, S], BF16)
    cT = dscr("cT", [1024, S], BF16)
    ynT = dscr("ynT", [2048, S], BF16)
    f2 = dscr("f2", [8, 128, FW], F32)
    acs_d = dscr("acs_d", [S, 32], F32)
    dec_d = dscr("dec_d", [S, 32], F32)
    etot_d = dscr("etot_d", [S // 2, 32], F32)

    kb = KB(nc)
    kb.begin()
    CF = kb.sb([128, C_END], F32, root=True)
    ident_bf = kb.sb([128, 128], BF16, root=True)
    ones_bf = kb.sb([128, 128], BF16, root=True)
    gp = kb.sb([128, 8], F32, root=True)
    nlam = kb.sb([128, 1], F32, root=True)
    gsub = kb.sb([128, 1], F32, root=True)
    cball = kb.sb([128, 8], F32, root=True)
    kb.share([CF, gp, gsub, cball])
    kb.I('sp', lambda e: e.dma_start(out=CF[:], in_=cst), w=[CF], dma=CF)
    kb.I('sp', lambda e: e.dma_start(out=gp[:], in_=gpre_l), w=[gp], dma=gp)
    kb.I('sp', lambda e: e.dma_start(out=gsub[:], in_=subln_l), w=[gsub], dma=gsub)
    kb.I('sp', lambda e: e.dma_start(out=cball[:], in_=bass.AP(tensor=rbT.tensor, offset=31, ap=[[0, 128], [32, 8]]),
                                     allow_slow_non_contiguous=True), w=[cball], dma=cball)
    kb.seal([CF, gp, gsub, cball])
    kb.I('dve', lambda e: e.tensor_copy(out=ident_bf[:], in_=CF[:, C_ID:C_ID + 128]), r=[CF], w=[ident_bf])
    kb.I('dve', lambda e: e.tensor_copy(out=ones_bf[:], in_=CF[:, C_ONE:C_ONE + 128]), r=[CF], w=[ones_bf])
    kb.I('dve', lambda e: e.tensor_scalar(out=gsub[:], in0=gsub[:], scalar1=float(1.0 - LAMBDA_INIT), scalar2=None,
                                          op0=ALU.mult), r=[gsub], w=[gsub])
    LV = kb.sb([128, 4, 64], F32)
    kb.I('sp', lambda e: e.dma_start(out=LV[:], in_=bass.AP(tensor=lamv.tensor, offset=0, ap=[[0, 128], [64, 4], [1, 64]])),
         w=[LV], dma=LV)
    LP = kb.sb([128, 2, 64], F32)
    LS = kb.sb([128, 2], F32)
    LE = kb.sb([128, 2], F32)
    kb.I('dve', lambda e: e.tensor_tensor(out=LP[:, 0, :], in0=LV[:, 0, :], in1=LV[:, 1, :], op=ALU.mult), r=[LV], w=[LP])
    kb.I('dve', lambda e: e.tensor_tensor(out=LP[:, 1, :], in0=LV[:, 2, :], in1=LV[:, 3, :], op=ALU.mult), r=[LV], w=[LP])
    kb.I('dve', lambda e: e.reduce_sum(out=LS[:], in_=LP[:], axis=mybir.AxisListType.X), r=[LP], w=[LS])
    kb.I('act', lambda e: e.activation(out=LE[:], in_=LS[:], func=AF.Exp), r=[LS], w=[LE])
    kb.I('dve', lambda e: e.tensor_tensor(out=nlam[:], in0=LE[:, 1:2], in1=LE[:, 0:1], op=ALU.subtract), r=[LE], w=[nlam])
    kb.I('dve', lambda e: e.tensor_scalar(out=nlam[:], in0=nlam[:], scalar1=float(-LAMBDA_INIT), scalar2=None, op0=ALU.add),
         r=[nlam], w=[nlam])
    RB = kb.sb([8, 32], F32)
    FV = kb.sb([8, FW], F32)
    kb.I('sp', lambda e: e.dma_start(out=RB[:], in_=rbT), w=[RB], dma=RB)
    kb.I('dve', lambda e: e.memset(FV[:, 0:FOFF], NEG), w=[FV])
    kb.I('dve', lambda e: e.tensor_copy(out=FV[:, FOFF:FOFF + 16], in_=RB[:, 0:16]), r=[RB], w=[FV])
    bk = t5_bucket_np(np.arange(FW - FOFF))
    for b in range(16, 32):
        idx = np.nonzero(bk == b)[0]
        if len(idx) == 0:
            continue
        lo, hi = int(idx[0]), int(idx[-1]) + 1
        assert np.all(bk[lo:hi] == b)
        src = RB[:, b:b + 1]
        kb.I('dve', lambda e: e.tensor_copy(out=FV[:, FOFF + lo:FOFF + hi], in_=bc_ap(src, [(0, hi - lo)])), r=[RB], w=[FV])
    fvs = FV[:]
    kb.I('sp', lambda e: e.dma_start(out=f2, in_=bass.AP(tensor=fvs.tensor, offset=fvs.offset, ap=[fvs.ap[0], [0, 128], [1, FW]])),
         r=[FV], dma=FV)
    kb.end()
    if upto == 0:
        kb.root.close()
        return nc, kb

    kb.begin()
    xr = [kb.sb([128, 4, 1024], F32) for _ in range(2)]
    sqj = kb.sb([128, 1024], F32)
    ssr = [kb.sb([128, 4], F32) for _ in range(2)]
    rsr = [kb.sb([128, 4], F32) for _ in range(2)]
    rir = [kb.sb([128, 4], F32) for _ in range(2)]
    hbr = [kb.sb([128, 4, 1024], BF16) for _ in range(2)]
    ptr = [kb.psum([128, 2, 512], BF16) for _ in range(8)]
    hor = [kb.sb([128, 8, 512], BF16) for _ in range(2)]
    hT_v = hT.rearrange("(c p) t -> p c t", p=128)
    for g in range(NQ):
        X, SS, RS, RI, H, HO = xr[g % 2], ssr[g % 2], rsr[g % 2], rir[g % 2], hbr[g % 2], hor[g % 2]
        kb.I('sp', lambda e: e.dma_start(out=X[:], in_=x_ap[g * 512:(g + 1) * 512, :].rearrange("(j p) d -> p j d", p=128)),
             w=[X], dma=X)
        for j in range(4):
            kb.I('act', lambda e: e.activation(out=sqj[:], in_=X[:, j, :], func=AF.Square, accum_out=SS[:, j:j + 1]),
                 r=[X], w=[sqj, SS])
        kb.I('act', lambda e: e.activation(out=RS[:], in_=SS[:], func=AF.Sqrt, scale=1.0 / D, bias=EPS), r=[SS], w=[RS])
        kb.I('dve', lambda e: e.reciprocal(out=RI[:], in_=RS[:]), r=[RS], w=[RI])
        for j in range(4):
            kb.I('dve',
                 lambda e: e.tensor_scalar(out=H[:, j, :], in0=X[:, j, :], scalar1=RI[:, j:j + 1], scalar2=None, op0=ALU.mult),
                 r=[X, RI], w=[H])
        for c2 in range(4):
            PT = ptr[(g * 4 + c2) % 8]
            for cc in range(2):
                c = c2 * 2 + cc
                for j in range(4):
                    kb.I('pe', lambda e: e.transpose(out=PT[:, cc, j * 128:(j + 1) * 128], in_=H[:, j, c * 128:(c + 1) * 128],
                                                     identity=ident_bf[:]), r=[H, ident_bf], w=[PT])
            if c2 % 2 == 0:
                kb.I('act', lambda e: e.copy(out=HO[:, c2 * 2:c2 * 2 + 2, :], in_=PT[:]), r=[PT], w=[HO])
            else:
                kb.I('dve', lambda e: e.tensor_copy(out=HO[:, c2 * 2:c2 * 2 + 2, :], in_=PT[:]), r=[PT], w=[HO])
        kb.I('sp', lambda e: e.dma_start(out=hT_v[:, :, g * 512:(g + 1) * 512], in_=HO[:]), r=[HO], dma=HO)
    kb.end()
    if upto == 1:
        kb.root.close()
        return nc, kb

    kb.begin()
    w_v = w_in.rearrange("(c p) n -> p c n", p=128)
    segs = [(0, 1024, qT, None), (1024, 1024, kT, None), (6144, 4096, xbcT, None),
            (3072, 1024, gT, AF.Silu), (4096, 2048, szT, AF.Silu),
            (10272, 1024, gaT, AF.Sigmoid), (11296, 1024, gsT, AF.Sigmoid)]
    blocks = []
    for c0, n, dst, fn in segs:
        for i in range(n // 128):
            blocks.append((c0 + i * 128, dst, i * 128, fn))
    G = 8
    wst = [kb.sb([128, 8, 128], F32) for _ in range(2)]
    wg = [kb.sb([128, 8, G * 128], BF16) for _ in range(2)]
    htr = [kb.sb([128, 8, 512], BF16) for _ in range(3)]
    pbr = [kb.psum([128, 512], F32) for _ in range(6)]
    outr = [kb.sb([128, 512], BF16) for _ in range(4)]
    gps = gp[:]
    nload = 0
    nps = 0
    nout = 0
    nws = 0
    ngroups = (len(blocks) + G - 1) // G

    def load_group(gi):
        nonlocal nws
        WG = wg[gi % 2]
        for bi, (c0, dst, r0, fn) in enumerate(blocks[gi * G:(gi + 1) * G]):
            WS = wst[nws % 2]
            nws += 1
            kb.I('sp', lambda e: e.dma_start(out=WS[:], in_=w_v[:, :, c0:c0 + 128]), w=[WS], dma=WS)
            kb.I('dve', lambda e: e.tensor_tensor(out=WG[:, :, bi * 128:(bi + 1) * 128], in0=WS[:],
                                                   in1=bc_ap(gps, [(1, 8), (0, 128)]), op=ALU.mult), r=[WS, gp], w=[WG])
    load_group(0)
    for gi in range(ngroups):
        if gi + 1 < ngroups:
            load_group(gi + 1)
        WG = wg[gi % 2]
        gblocks = blocks[gi * G:(gi + 1) * G]
        for tq in range(NQ):
            HT = htr[nload % 3]
            nload += 1
            kb.I('sp', lambda e: e.dma_start(out=HT[:], in_=hT_v[:, :, tq * 512:(tq + 1) * 512]), w=[HT], dma=HT)
            for bi, (c0, dst, r0, fn) in enumerate(gblocks):
                PS = pbr[nps % 6]
                nps += 1
                for c in range(8):
                    kb.I('pe', lambda e: e.matmul(PS[:], lhsT=WG[:, c, bi * 128:(bi + 1) * 128], rhs=HT[:, c, :],
                                                  start=(c == 0), stop=(c == 7)), r=[WG, HT], w=[PS])
                OU = outr[nout % 4]
                nout += 1
                if fn is None:
                    kb.I('dve', lambda e: e.tensor_copy(out=OU[:], in_=PS[:]), r=[PS], w=[OU])
                else:
                    kb.I('act', lambda e: e.activation(out=OU[:], in_=PS[:], func=fn), r=[PS], w=[OU])
                kb.I('sp',
                     lambda e: e.dma_start(out=dst[r0:r0 + 128, tq * 512:(tq + 1) * 512], in_=OU[:]), r=[OU], dma=OU)
    kb.end()
    if upto == 2:
        kb.root.close()
        return nc, kb

    kb.begin()
    wv = kb.sb([128, 8, 1024], BF16)
    wdt = kb.sb([128, 8, 32], BF16)
    wst2 = [kb.sb([128, 8, 256], F32) for _ in range(2)]
    gps = gp[:]
    for i in range(4):
        WS = wst2[i % 2]
        kb.I('sp', lambda e: e.dma_start(out=WS[:], in_=w_v[:, :, 2048 + i * 256:2048 + (i + 1) * 256]), w=[WS], dma=WS)
        kb.I('dve', lambda e: e.tensor_tensor(out=wv[:, :, i * 256:(i + 1) * 256], in0=WS[:],
                                               in1=bc_ap(gps, [(1, 8), (0, 256)]), op=ALU.mult), r=[WS, gp], w=[wv])
    WS = wst2[0]
    kb.I('sp', lambda e: e.dma_start(out=WS[:, :, 0:32], in_=w_v[:, :, 10240:10272]), w=[WS], dma=WS)
    kb.I('dve', lambda e: e.tensor_tensor(out=wdt[:], in0=WS[:, :, 0:32], in1=bc_ap(gps, [(1, 8), (0, 32)]), op=ALU.mult),
         r=[WS, gp], w=[wdt])
    DTB = kb.sb([128, 32], F32)
    AB = kb.sb([128, 32], F32)
    kb.I('sp', lambda e: e.dma_start(out=DTB[:], in_=bass.AP(tensor=dtb.tensor, offset=0, ap=[[0, 128], [1, 32]])), w=[DTB], dma=DTB)
    kb.I('sp', lambda e: e.dma_start(out=AB[:], in_=bass.AP(tensor=alog.tensor, offset=0, ap=[[0, 128], [1, 32]])), w=[AB], dma=AB)
    kb.I('act', lambda e: e.activation(out=AB[:], in_=AB[:], func=AF.Exp), r=[AB], w=[AB])
    kb.I('dve', lambda e: e.tensor_scalar(out=AB[:], in0=AB[:], scalar1=-1.0, scalar2=None, op0=ALU.mult), r=[AB], w=[AB])
    htr = [kb.sb([128, 8, 512], BF16) for _ in range(2)]
    pvr = [kb.psum([128, 512], F32) for _ in range(4)]
    pdr = [kb.psum([128, 32], F32) for _ in range(2)]
    vor = [kb.sb([128, 4, 1024], BF16) for _ in range(2)]
    dtr = [kb.sb([128, 4, 32], F32) for _ in range(2)]
    atr = [kb.sb([128, 4, 32], F32) for _ in range(2)]
    tmp = [kb.sb([128, 32], F32) for _ in range(2)]
    npv = 0
    for tq in range(NQ):
        HT = htr[tq % 2]
        VO, DT, AT = vor[tq % 2], dtr[tq % 2], atr[tq % 2]
        kb.I('sp', lambda e: e.dma_start(out=HT[:], in_=hT_v[:, :, tq * 512:(tq + 1) * 512]), w=[HT], dma=HT)
        for ts in range(4):
            for half in range(2):
                PS = pvr[npv % 4]
                npv += 1
                for c in range(8):
                    kb.I('pe', lambda e: e.matmul(PS[:], lhsT=HT[:, c, ts * 128:(ts + 1) * 128], rhs=wv[:, c, half * 512:(half + 1) * 512],
                                                  start=(c == 0), stop=(c == 7)), r=[HT, wv], w=[PS])
                if half == 0:
                    kb.I('dve', lambda e: e.tensor_copy(out=VO[:, ts, 0:512], in_=PS[:]), r=[PS], w=[VO])
                else:
                    kb.I('act', lambda e: e.copy(out=VO[:, ts, 512:1024], in_=PS[:]), r=[PS], w=[VO])
            PD = pdr[ts % 2]
            TM = tmp[ts % 2]
            for c in range(8):
                kb.I('pe', lambda e: e.matmul(PD[:], lhsT=HT[:, c, ts * 128:(ts + 1) * 128], rhs=wdt[:, c, :],
                                              start=(c == 0), stop=(c == 7)), r=[HT, wdt], w=[PD])
            kb.I('dve', lambda e: e.tensor_tensor(out=TM[:], in0=PD[:], in1=DTB[:], op=ALU.add), r=[PD, DTB], w=[TM])
            kb.I('act', lambda e: e.activation(out=TM[:], in_=TM[:], func=AF.Exp), r=[TM], w=[TM])
            kb.I('act', lambda e: e.activation(out=DT[:, ts, :], in_=TM[:], func=AF.Ln, bias=1.0), r=[TM], w=[DT])
            kb.I('dve', lambda e: e.tensor_tensor(out=AT[:, ts, :], in0=DT[:, ts, :], in1=AB[:], op=ALU.mult), r=[DT, AB], w=[AT])
        kb.I('sp', lambda e: e.dma_start(out=v_tm[tq * 512:(tq + 1) * 512, :].rearrange("(j p) n -> p j n", p=128), in_=VO[:]),
             r=[VO], dma=VO)
        kb.I('sp', lambda e: e.dma_start(out=dt_tm[tq * 512:(tq + 1) * 512, :].rearrange("(j p) n -> p j n", p=128), in_=DT[:]),
             r=[DT], dma=DT)
        kb.I('sp', lambda e: e.dma_start(out=a_tm[tq * 512:(tq + 1) * 512, :].rearrange("(j p) n -> p j n", p=128), in_=AT[:]),
             r=[AT], dma=AT)
    kb.end()
    if upto == 3:
        kb.root.close()
        return nc, kb

    kb.begin()
    ktz = [kb.sb([128, S], BF16) for _ in range(2)]
    QT = kb.sb([128, S], BF16)
    V = kb.sb([128, NT, 128], BF16)
    BO = kb.sb([128, 5, 512], F32)
    kb.I('dve', lambda e: e.memset(ktz[0][64:128, :], 0.0), w=[ktz[0]])
    kb.I('dve', lambda e: e.memset(ktz[1][0:64, :], 0.0), w=[ktz[1]])
    psr = [kb.psum([128, 512], F32) for _ in range(4)]
    po = [kb.psum([128, 512], F32) for _ in range(2)]
    psu = [kb.psum([128, 512], F32) for _ in range(2)]
    ptr_ = [kb.sb([128, 512], BF16) for _ in range(4)]
    sbr = [kb.sb([128, 512], F32) for _ in range(2)]
    gtr = [kb.sb([128, 512], BF16) for _ in range(2)]
    ogr = [kb.sb([128, 512], BF16) for _ in range(2)]
    r0t = kb.sb([128, 512], F32)
    r1t = kb.sb([128, 512], F32)
    t0t = kb.sb([128, 512], F32)
    t1t = kb.sb([128, 512], F32)
    ot = kb.sb([128, 512], F32)
    sqt = kb.sb([128, 512], F32)
    sqh = kb.sb([128, 512], BF16)
    sql = kb.sb([128, 512], BF16)
    rnt = kb.sb([128, 512], F32)
    rit = kb.sb([128, 512], F32)
    ont = kb.sb([128, 512], F32)
    scale = 0.125
    nst = 0
    npt = 0
    nsb = 0
    nqt = 0

    def load_head(h):
        for m in range(2):
            KZ = ktz[m]
            kb.I('sp', lambda e: e.dma_start(out=KZ[m * 64:(m + 1) * 64, :], in_=kT[h * 128 + m * 64:h * 128 + (m + 1) * 64, :]),
                 w=[KZ], dma=KZ)
        kb.I('sp', lambda e: e.dma_start(out=QT[:], in_=qT[h * 128:(h + 1) * 128, :]), w=[QT], dma=QT)
        kb.I('sp', lambda e: e.dma_start(out=V[:], in_=v_tm[:, h * 128:(h + 1) * 128].rearrange("(n p) e -> p n e", p=128)),
             w=[V], dma=V)
        for oi, off in enumerate(range(-1, 4)):
            src = bass.AP(tensor=f2.tensor, offset=f2.offset + h * 128 * FW + FOFF - 128 * off, ap=[[FW - 1, 128], [1, 512]])
            kb.I('sp', lambda e: e.dma_start(out=BO[:, oi, :], in_=src), w=[BO], dma=BO)
    for h in range(NH):
        load_head(h)
        for qt in range(NQ):
            GT = gtr[nqt % 2]
            OG = ogr[nqt % 2]
            nqt += 1
            kb.I('sp', lambda e: e.dma_start(out=GT[:], in_=gT[h * 128:(h + 1) * 128, qt * 512:(qt + 1) * 512]), w=[GT], dma=GT)
            steps = [(j, None, 0) for j in range(0, 4 * qt - 1)]
            for oi, off in enumerate(range(-1, 4)):
                j = 4 * qt + off
                if j < 0:
                    continue
                steps.append((j, oi, max(0, 128 * off)))
            for si, (j, oi, lo) in enumerate(steps if CDBG >= 2 else []):
                first = si == 0
                last = si == len(steps) - 1
                for m in range(2):
                    PS = psr[nst % 4]
                    nst += 1
                    PT = ptr_[npt % 4]
                    npt += 1
                    kb.I('pe', lambda e: e.matmul(PS[:, lo:512], lhsT=ktz[m][:, j * 128:(j + 1) * 128],
                                                  rhs=QT[:, qt * 512 + lo:(qt + 1) * 512], start=True, stop=True),
                         r=[ktz[m], QT], w=[PS])
                    if oi is None:
                        kb.I('act', lambda e: e.activation(out=PT[:, lo:512], in_=PS[:, lo:512], func=AF.Exp,
                                                           bias=cball[:, h:h + 1], scale=scale), r=[PS, cball], w=[PT])
                    else:
                        SB = sbr[nsb % 2]
                        nsb += 1
                        kb.I('dve', lambda e: e.scalar_tensor_tensor(out=SB[:, lo:512], in0=PS[:, lo:512], scalar=scale,
                                                                     in1=BO[:, oi, lo:512], op0=ALU.mult, op1=ALU.add),
                             r=[PS, BO], w=[SB])
                        kb.I('act', lambda e: e.activation(out=PT[:, lo:512], in_=SB[:, lo:512], func=AF.Exp), r=[SB], w=[PT])
                    if CDBG < 3:
                        continue
                    kb.I('pe', lambda e: e.matmul(po[m][:, lo:512], lhsT=V[:, j, :], rhs=PT[:, lo:512], start=first, stop=last),
                         r=[V, PT], w=[po[m]])
                    kb.I('pe', lambda e: e.matmul(psu[m][:, lo:512], lhsT=ones_bf[:], rhs=PT[:, lo:512], start=first, stop=last),
                         r=[ones_bf, PT], w=[psu[m]])
            if CDBG < 4:
                continue
            kb.I('dve', lambda e: e.reciprocal(out=r0t[:], in_=psu[0][:]), r=[psu[0]], w=[r0t])
            kb.I('dve', lambda e: e.reciprocal(out=r1t[:], in_=psu[1][:]), r=[psu[1]], w=[r1t])
            kb.I('dve', lambda e: e.tensor_tensor(out=t0t[:], in0=po[0][:], in1=r0t[:], op=ALU.mult), r=[po[0], r0t], w=[t0t])
            kb.I('dve', lambda e: e.tensor_tensor(out=t1t[:], in0=po[1][:], in1=r1t[:], op=ALU.mult), r=[po[1], r1t], w=[t1t])
            kb.I('dve', lambda e: e.scalar_tensor_tensor(out=ot[:], in0=t1t[:], scalar=nlam[:, 0:1], in1=t0t[:],
                                                         op0=ALU.mult, op1=ALU.add), r=[t1t, t0t, nlam], w=[ot])
            kb.I('act', lambda e: e.activation(out=sqt[:], in_=ot[:], func=AF.Square), r=[ot], w=[sqt])
            PN = psr[nst % 4]
            nst += 1
            kb.I('dve', lambda e: e.tensor_copy(out=sqh[:], in_=sqt[:]), r=[sqt], w=[sqh])
            kb.I('dve', lambda e: e.tensor_tensor(out=sql[:], in0=sqt[:], in1=sqh[:], op=ALU.subtract), r=[sqt, sqh], w=[sql])
            kb.I('pe', lambda e: e.matmul(PN[:], lhsT=ones_bf[:], rhs=sqh[:], start=True, stop=False), r=[ones_bf, sqh], w=[PN])
            kb.I('pe', lambda e: e.matmul(PN[:], lhsT=ones_bf[:], rhs=sql[:], start=False, stop=True), r=[ones_bf, sql], w=[PN])
            kb.I('act', lambda e: e.activation(out=rnt[:], in_=PN[:], func=AF.Sqrt, scale=1.0 / 128, bias=EPS), r=[PN], w=[rnt])
            kb.I('dve', lambda e: e.reciprocal(out=rit[:], in_=rnt[:]), r=[rnt], w=[rit])
            kb.I('dve', lambda e: e.scalar_tensor_tensor(out=ont[:], in0=ot[:], scalar=gsub[:, 0:1], in1=rit[:],
                                                         op0=ALU.mult, op1=ALU.mult), r=[ot, rit, gsub], w=[ont])
            kb.I('dve', lambda e: e.tensor_tensor(out=OG[:], in0=ont[:], in1=GT[:], op=ALU.mult), r=[ont, GT], w=[OG])
            kb.I('sp', lambda e: e.dma_start(out=ogT[h * 128:(h + 1) * 128, qt * 512:(qt + 1) * 512], in_=OG[:]), r=[OG], dma=OG)
    kb.end()
    if upto == 4:
        kb.root.close()
        return nc, kb

    kb.begin()
    TB = min(S, 2048)
    NB = S // TB
    CW = kb.sb([128, 32, 4], F32)
    CBI = kb.sb([128, 32], F32)
    kb.I('sp', lambda e: e.dma_start(out=CW[:], in_=convw_l), w=[CW], dma=CW)
    kb.I('sp', lambda e: e.dma_start(out=CBI[:], in_=convb_l), w=[CBI], dma=CBI)
    ur = [kb.sb([128, TB + 4], BF16) for _ in range(2)]
    accr = [kb.sb([128, TB], F32) for _ in range(2)]
    cor = [kb.sb([128, TB], BF16) for _ in range(2)]
    ptc = [kb.psum([128, 4, 128], BF16) for _ in range(4)]
    tor = [kb.sb([128, TB // 128, 128], BF16) for _ in range(2)]
    nu = 0
    npc = 0
    nto = 0
    for cc in range(32):
        for tb in range(NB):
            U, ACC, CO = ur[nu % 2], accr[nu % 2], cor[nu % 2]
            nu += 1
            t0 = tb * TB
            if tb == 0:
                kb.I('dve', lambda e: e.memset(U[:, 0:4], 0.0), w=[U])
                kb.I('sp', lambda e: e.dma_start(out=U[:, 4:4 + TB], in_=xbcT[cc * 128:(cc + 1) * 128, 0:TB]), w=[U], dma=U)
            else:
                kb.I('sp', lambda e: e.dma_start(out=U[:], in_=xbcT[cc * 128:(cc + 1) * 128, t0 - 4:t0 + TB]), w=[U], dma=U)
            kb.I('dve', lambda e: e.tensor_scalar(out=ACC[:], in0=U[:, 1:1 + TB], scalar1=CW[:, cc, 0:1], scalar2=None, op0=ALU.mult),
                 r=[U, CW], w=[ACC])
            for j in range(1, 4):
                eng = 'dve'
                kb.I(eng, lambda e: e.scalar_tensor_tensor(out=ACC[:], in0=U[:, 1 + j:1 + j + TB], scalar=CW[:, cc, j:j + 1],
                                                           in1=ACC[:], op0=ALU.mult, op1=ALU.add), r=[U, CW, ACC], w=[ACC])
            kb.I('act', lambda e: e.activation(out=CO[:], in_=ACC[:], func=AF.Silu, bias=CBI[:, cc:cc + 1]), r=[ACC, CBI], w=[CO])
            if cc >= 16:
                dstT = bT if cc < 24 else cT
                r0 = (cc - 16) * 128 if cc < 24 else (cc - 24) * 128
                kb.I('sp', lambda e: e.dma_start(out=dstT[r0:r0 + 128, t0:t0 + TB], in_=CO[:]), r=[CO], dma=CO)
            if cc < 24:
                TO = tor[nto % 2]
                nto += 1
                for q4 in range(TB // 512):
                    PT = ptc[npc % 4]
                    npc += 1
                    for k in range(4):
                        kb.I('pe', lambda e: e.transpose(out=PT[:, k, :], in_=CO[:, (q4 * 4 + k) * 128:(q4 * 4 + k + 1) * 128],
                                                         identity=ident_bf[:]), r=[CO, ident_bf], w=[PT])
                    if q4 % 2 == 0:
                        kb.I('dve', lambda e: e.tensor_copy(out=TO[:, q4 * 4:q4 * 4 + 4, :], in_=PT[:]), r=[PT], w=[TO])
                    else:
                        kb.I('act', lambda e: e.copy(out=TO[:, q4 * 4:q4 * 4 + 4, :], in_=PT[:]), r=[PT], w=[TO])
                if cc < 16:
                    dtm, c0 = xs_tm, cc * 128
                else:
                    dtm, c0 = b_tm, (cc - 16) * 128
                kb.I('sp', lambda e: e.dma_start(out=dtm[t0:t0 + TB, c0:c0 + 128].rearrange("(n p) c -> p n c", p=128), in_=TO[:]),
                     r=[TO], dma=TO)
    kb.end()
    if upto == 5:
        kb.root.close()
        return nc, kb

    kb.begin()
    a5 = [kb.sb([128, 2, 32], F32) for _ in range(2)]
    p5 = [kb.psum([128, 4, 128], F32) for _ in range(2)]
    acs5 = [kb.sb([128, 2, 32], F32) for _ in range(2)]
    dec5 = [kb.sb([128, 2, 32], F32) for _ in range(2)]
    et5 = [kb.sb([128, 32], F32) for _ in range(2)]
    ap5 = [[kb.sb([128, 2, 32], BF16) for _ in range(3)] for _ in range(2)]
    r5a = kb.sb([128, 2, 32], F32)
    s5r = [kb.sb([128, 4, 32], F32) for _ in range(2)]
    r5b = kb.sb([128, 2, 32], F32)
    cbf5 = kb.sb([128, 3, 128], BF16)
    kb.I('dve', lambda e: e.tensor_copy(out=cbf5[:, 0, :], in_=CF[:, C_TRI0:C_TRI0 + 128]), r=[CF], w=[cbf5])
    kb.I('dve', lambda e: e.tensor_copy(out=cbf5[:, 1, :], in_=CF[:, C_ONE:C_ONE + 128]), r=[CF], w=[cbf5])
    kb.I('dve', lambda e: e.tensor_copy(out=cbf5[:, 2, :], in_=CF[:, C_SUT:C_SUT + 128]), r=[CF], w=[cbf5])
    for ch in range(NCH):
        t0 = ch * 256
        A5, P5, AC5, DE5, ET5 = a5[ch % 2], p5[ch % 2], acs5[ch % 2], dec5[ch % 2], et5[ch % 2]
        kb.I('sp', lambda e: e.dma_start(out=A5[:], in_=a_tm[t0:t0 + 256, :].rearrange("(j p) n -> p j n", p=128)), w=[A5], dma=A5)
        if E15 < 1:
            continue
        P3 = ap5[ch % 2]
        kb.I('dve', lambda e: e.tensor_copy(out=P3[0][:], in_=A5[:]), r=[A5], w=[P3[0]])
        kb.I('dve', lambda e: e.tensor_tensor(out=r5a[:], in0=A5[:], in1=P3[0][:], op=ALU.subtract), r=[A5, P3[0]], w=[r5a])
        kb.I('dve', lambda e: e.tensor_copy(out=P3[1][:], in_=r5a[:]), r=[r5a], w=[P3[1]])
        kb.I('dve', lambda e: e.tensor_tensor(out=r5b[:], in0=r5a[:], in1=P3[1][:], op=ALU.subtract), r=[r5a, P3[1]], w=[r5b])
        kb.I('dve', lambda e: e.tensor_copy(out=P3[2][:], in_=r5b[:]), r=[r5b], w=[P3[2]])

        def mmg5(o, terms):
            n = len(terms) * 3
            i = 0
            for l, sc_ in terms:
                for pc in range(3):
                    PA_ = P3[pc]
                    kb.I('pe', lambda e: e.matmul(o, lhsT=l, rhs=PA_[:, sc_, :], start=(i == 0), stop=(i == n - 1)),
                         r=[cbf5, PA_], w=[P5])
                    i += 1
        TRIb5, ONEb5, SUTb5 = cbf5[:, 0, :], cbf5[:, 1, :], cbf5[:, 2, :]
        mmg5(P5[:, 0, 0:32], [(TRIb5, 0)])
        mmg5(P5[:, 1, 0:32], [(ONEb5, 0), (TRIb5, 1)])
        mmg5(P5[:, 2, 0:32], [(SUTb5, 0), (ONEb5, 1)])
        mmg5(P5[:, 3, 0:32], [(SUTb5, 1)])
        if E15 < 2:
            continue
        S5 = s5r[ch % 2]
        kb.I('dve', lambda e: e.tensor_copy(out=S5[:], in_=P5[:, :, 0:32]), r=[P5], w=[S5])
        kb.I('dve', lambda e: e.tensor_copy(out=AC5[:], in_=S5[:, 0:2, :]), r=[S5], w=[AC5])
        kb.I('act', lambda e: e.activation(out=DE5[:], in_=S5[:, 2:4, :], func=AF.Exp), r=[S5], w=[DE5])
        kb.I('dve', lambda e: e.tensor_tensor(out=ET5[:], in0=S5[:, 2, :], in1=S5[:, 0, :], op=ALU.add), r=[S5], w=[ET5])
        kb.I('act', lambda e: e.activation(out=ET5[:], in_=ET5[:], func=AF.Exp), r=[ET5], w=[ET5])
        if E15 < 3:
            continue
        kb.I('sp', lambda e: e.dma_start(out=acs_d[t0:t0 + 256, :].rearrange("(j p) n -> p j n", p=128), in_=AC5[:]), r=[AC5], dma=AC5)
        kb.I('sp', lambda e: e.dma_start(out=dec_d[t0:t0 + 256, :].rearrange("(j p) n -> p j n", p=128), in_=DE5[:]), r=[DE5], dma=DE5)
        kb.I('sp', lambda e: e.dma_start(out=etot_d[ch * 128:(ch + 1) * 128, :], in_=ET5[:]), r=[ET5], dma=ET5)
    kb.end()

    if upto == 55:
        kb.root.close()
        return nc, kb
    kb.begin()
    ab_r = [kb.sb([128, 3, 2, 8, 128], BF16)]
    DSK = kb.sb([128, 32], F32)
    GN = kb.sb([128, 16], F32)
    DI = kb.sb([128, 32, 128], BF16)
    kb.I('sp', lambda e: e.dma_start(out=DSK[:], in_=bass.AP(tensor=dsk.tensor, offset=0, ap=[[0, 128], [1, 32]])), w=[DSK], dma=DSK)
    kb.I('sp', lambda e: e.dma_start(out=GN[:], in_=gn_l), w=[GN], dma=GN)
    for hd in range(32):
        kb.I('dve',
             lambda e: e.tensor_scalar(out=DI[:, hd, :], in0=CF[:, C_ID:C_ID + 128], scalar1=DSK[:, hd:hd + 1], scalar2=None,
                                       op0=ALU.mult), r=[CF, DSK], w=[DI])
    stf = kb.sb([128, 32, 64], F32)
    stb = kb.sb([128, 32, 64], BF16)
    kb.I('dve', lambda e: e.memset(stf[:], 0.0), w=[stf])
    kb.I('dve', lambda e: e.memset(stb[:], 0.0), w=[stb])
    a_r = [kb.sb([128, 2, 32], F32) for _ in range(2)]
    ap_r = [[kb.sb([128, 2, 32], BF16) for _ in range(3)] for _ in range(2)]
    rr_r = [kb.sb([128, 2, 32], F32) for _ in range(2)]
    cbf = kb.sb([128, 3, 128], BF16)
    kb.I('dve', lambda e: e.tensor_copy(out=cbf[:, 0, :], in_=CF[:, C_TRI0:C_TRI0 + 128]), r=[CF], w=[cbf])
    kb.I('dve', lambda e: e.tensor_copy(out=cbf[:, 1, :], in_=CF[:, C_ONE:C_ONE + 128]), r=[CF], w=[cbf])
    kb.I('dve', lambda e: e.tensor_copy(out=cbf[:, 2, :], in_=CF[:, C_SUT:C_SUT + 128]), r=[CF], w=[cbf])
    d_r = [kb.sb([128, 2, 32], F32) for _ in range(2)]
    xs_r = [kb.sb([128, 2, 2048], BF16) for _ in range(2)]
    bt_r = [kb.sb([128, 2, 1024], BF16) for _ in range(2)]
    bT_r = [kb.sb([128, 8, 256], BF16) for _ in range(2)]
    cT_r = [kb.sb([128, 8, 256], BF16) for _ in range(2)]
    sz_r = [kb.sb([128, 16, 256], BF16) for _ in range(2)]
    X_r = [kb.sb([128, 2, 2048], BF16) for _ in range(2)]
    Xd_r = [kb.sb([128, 2, 2048], BF16) for _ in range(2)]
    yn_r = [kb.sb([128, 16, 256], BF16) for _ in range(2)]
    pcb = [kb.psum([128, 2, 256], F32) for _ in range(2)]
    pbc = [kb.psum([128, 256], F32) for _ in range(2)]
    py = [kb.psum([128, 256], F32) for _ in range(2)]
    pst = [kb.psum([128, 64], F32)]
    pss = kb.psum([128, 256], F32)
    acs = [kb.sb([128, 2, 32], F32) for _ in range(2)]
    dec = [kb.sb([128, 2, 32], F32) for _ in range(2)]
    etot = [kb.sb([128, 32], F32) for _ in range(2)]
    el_r = [kb.sb([128, 256], F32) for _ in range(2)]
    sg_r = [kb.sb([128, 256], F32) for _ in range(4)]
    lt_r = [kb.sb([128, 256], F32) for _ in range(4)]
    mt_r = [kb.sb([128, 2, 256], BF16) for _ in range(2)]
    cs_r = [kb.sb([128, 256], BF16) for _ in range(2)]
    yg_r = [kb.sb([128, 2, 256], F32) for _ in range(2)]
    sq_r = [kb.sb([128, 2, 256], F32) for _ in range(2)]
    sqh_r = [kb.sb([128, 2, 256], BF16) for _ in range(2)]
    sql_r = [kb.sb([128, 2, 256], BF16) for _ in range(2)]
    rn_t = kb.sb([128, 256], F32)
    ri_t = kb.sb([128, 256], F32)
    tri_bf = kb.sb([128, 512], BF16)
    kb.I('dve', lambda e: e.tensor_copy(out=tri_bf[:], in_=CF[:, C_TRI0:C_TRI0 + 512]), r=[CF], w=[tri_bf])
    ONE = lambda: CF[:, C_ONE:C_ONE + 128]
    nbc = 0
    nsg = 0
    npy = 0
    npst = 0
    for ch in range(NCH if DPRE >= 1 else 0):
        t0 = ch * 256
        k2 = ch % 2
        A_, D_, XS, BTM, BT_, CT_, SZ, X_, XD, YN = (a_r[k2], d_r[k2], xs_r[k2], bt_r[k2], bT_r[k2], cT_r[k2], sz_r[k2],
                                                     X_r[k2], Xd_r[k2], yn_r[k2])
        ACS, DEC, ETOT = acs[k2], dec[k2], etot[k2]
        lset = [A_, D_, XS, BTM, BT_, CT_, SZ]
        lset2 = [ACS, DEC, ETOT]
        kb.share(lset)
        kb.share(lset2)
        kb.I('sp', lambda e: e.dma_start(out=A_[:], in_=a_tm[t0:t0 + 256, :].rearrange("(j p) n -> p j n", p=128)), w=[A_], dma=A_)
        kb.I('sp', lambda e: e.dma_start(out=D_[:], in_=dt_tm[t0:t0 + 256, :].rearrange("(j p) n -> p j n", p=128)), w=[D_], dma=D_)
        kb.I('sp', lambda e: e.dma_start(out=XS[:], in_=xs_tm[t0:t0 + 256, :].rearrange("(j p) n -> p j n", p=128)), w=[XS], dma=XS)
        kb.I('sp', lambda e: e.dma_start(out=BTM[:], in_=b_tm[t0:t0 + 256, :].rearrange("(j p) n -> p j n", p=128)), w=[BTM], dma=BTM)
        kb.I('sp', lambda e: e.dma_start(out=BT_[:], in_=bT[:, t0:t0 + 256].rearrange("(g n) t -> n g t", n=128)), w=[BT_], dma=BT_)
        kb.I('sp', lambda e: e.dma_start(out=CT_[:], in_=cT[:, t0:t0 + 256].rearrange("(g n) t -> n g t", n=128)), w=[CT_], dma=CT_)
        for hf in range(2):
            kb.I('sp', lambda e: e.dma_start(out=SZ[:, hf * 8:(hf + 1) * 8, :],
                                             in_=szT[hf * 1024:(hf + 1) * 1024, t0:t0 + 256].rearrange("(c p) t -> p c t", p=128)), w=[SZ], dma=SZ)
        kb.seal(lset)
        if DPRE < 2:
            continue
        ds_ = D_[:]
        for sc in range(2):
            dsl = D_[:, sc, :]
            kb.I('dve',
                 lambda e: e.tensor_tensor(out=X_[:, sc, :].rearrange("p (h d) -> p h d", d=64),
                                           in0=XS[:, sc, :].rearrange("p (h d) -> p h d", d=64),
                                           in1=bc_ap(dsl, [(1, 32), (0, 64)]), op=ALU.mult), r=[XS, D_], w=[X_])
        TRI = CF[:, C_TRI0:C_TRI0 + 128]
        SUT = CF[:, C_SUT:C_SUT + 128]
        if DPRE < 3:
            continue
        AP3 = ap_r[k2]
        R1, R2 = rr_r[0], rr_r[1]
        kb.I('dve', lambda e: e.tensor_copy(out=AP3[0][:], in_=A_[:]), r=[A_], w=[AP3[0]])
        kb.I('dve', lambda e: e.tensor_tensor(out=R1[:], in0=A_[:], in1=AP3[0][:], op=ALU.subtract), r=[A_, AP3[0]], w=[R1])
        kb.I('dve', lambda e: e.tensor_copy(out=AP3[1][:], in_=R1[:]), r=[R1], w=[AP3[1]])
        kb.I('dve', lambda e: e.tensor_tensor(out=R2[:], in0=R1[:], in1=AP3[1][:], op=ALU.subtract), r=[R1, AP3[1]], w=[R2])
        kb.I('dve', lambda e: e.tensor_copy(out=AP3[2][:], in_=R2[:]), r=[R2], w=[AP3[2]])
        TRIb, ONEb, SUTb = cbf[:, 0, :], cbf[:, 1, :], cbf[:, 2, :]

        def mmg(o, terms):
            if DDBG4 == 30:
                return
            n = len(terms) * 3
            i = 0
            for l, sc_ in terms:
                for pc in range(3):
                    PA_ = AP3[pc]
                    kb.I('pe', lambda e: e.matmul(o, lhsT=l, rhs=PA_[:, sc_, :], start=(i == 0 or DDBG4 == 60), stop=(i == n - 1 or DDBG4 == 60)),
                         r=[cbf, PA_], w=[psm])
                    i += 1
        kb.I('sp', lambda e: e.dma_start(out=ACS[:], in_=acs_d[t0:t0 + 256, :].rearrange("(j p) n -> p j n", p=128)), w=[ACS], dma=ACS)
        kb.I('sp', lambda e: e.dma_start(out=DEC[:], in_=dec_d[t0:t0 + 256, :].rearrange("(j p) n -> p j n", p=128)), w=[DEC], dma=DEC)
        kb.I('sp', lambda e: e.dma_start(out=ETOT[:], in_=etot_d[ch * 128:(ch + 1) * 128, :]), w=[ETOT], dma=ETOT)
        kb.seal(lset2)
        for sc in range(2):
            dcl = DEC[:, sc, :]
            kb.I('dve',
                 lambda e: e.tensor_tensor(out=XD[:, sc, :].rearrange("p (h d) -> p h d", d=64),
                                           in0=X_[:, sc, :].rearrange("p (h d) -> p h d", d=64),
                                           in1=bc_ap(dcl, [(1, 32), (0, 64)]), op=ALU.mult), r=[X_, DEC], w=[XD])
        if DDBG4 == 50:
            kb.barrier()
        for g in range(8 if DDBG >= 2 else 0):
            PCB = pcb[g % 2]
            for sc in range(2 if DDBG3 & 1 else 0):
                kb.I('pe', lambda e: e.matmul(PCB[:, sc, :], lhsT=BT_[:, g, sc * 128:(sc + 1) * 128], rhs=CT_[:, g, :],
                                              start=True, stop=True), r=[BT_, CT_], w=[PCB])
            YG = yg_r[g % 2]
            SQ = sq_r[g % 2]
            ABt = ab_r[0]
            if g % 2 == 0:
                ncp = 0
                for pc in range(3):
                    for kc in range(2):
                        asl = AP3[pc][:, kc, g * 4:g * 4 + 8]
                        PA_ = AP3[pc]
                        kb.I('dve',
                             lambda e: e.tensor_copy(out=ABt[:, pc, kc, :, :], in_=bc_ap(asl, [(1, 8), (0, 128)])), r=[PA_], w=[ABt])
                        ncp += 1
            for hp in range(2):
                PYS = [py[0], py[1]]
                npy += 1
                pr = g * 2 + hp
                for hh in range(2):
                    hd = g * 4 + hp * 2 + hh
                    PY = PYS[hh]
                    PBC = pbc[nbc % 2]
                    EL = el_r[nbc % 2]
                    MT = mt_r[nbc % 2]
                    CS = cs_r[nbc % 2]
                    nbc += 1
                    ii = 0
                    for pc in range(3):
                        for kc in range(2):
                            kb.I('pe', lambda e: e.matmul(PBC[:], lhsT=(ABt[:, pc, kc, hd % 8, :] if DDBG4 != 70 else ones_bf[:]), rhs=tri_bf[:, kc * 256:(kc + 1) * 256],
                                                          start=(ii == 0), stop=(ii == 5)), r=[ABt, tri_bf], w=[PBC])
                            ii += 1
                    if DDBG2 < 1:
                        continue
                    kb.I('act', lambda e: e.activation(out=EL[:], in_=PBC[:], func=AF.Exp), r=[PBC], w=[EL])
                    for sc in range(2 if DDBG2 >= 2 else 0):
                        SG = sg_r[nsg % 4]
                        LT = lt_r[nsg % 4]
                        nsg += 1
                        kb.I('dve', lambda e: e.scalar_tensor_tensor(out=SG[:], in0=PBC[:], scalar=ACS[:, sc, hd:hd + 1],
                                                                     in1=CF[:, C_MK0 + sc * 256:C_MK0 + (sc + 1) * 256],
                                                                     op0=ALU.subtract, op1=ALU.add), r=[PBC, ACS, CF, EL], w=[SG])
                        kb.I('act', lambda e: e.activation(out=LT[:], in_=SG[:], func=AF.Exp), r=[SG], w=[LT])
                        kb.I('dve', lambda e: e.tensor_tensor(out=MT[:, sc, :], in0=PCB[:, sc, :], in1=LT[:], op=ALU.mult),
                             r=[PCB, LT], w=[MT])
                    if DDBG2 >= 3:
                        kb.I('dve', lambda e: e.tensor_tensor(out=CS[:], in0=CT_[:, g, :], in1=EL[:], op=ALU.mult), r=[CT_, EL], w=[CS])
                    if DDBG < 3:
                        continue
                    yo = PY[:]
                    kb.I('pe', lambda e: e.matmul(yo, lhsT=X_[:, 0, pr * 128:(pr + 1) * 128], rhs=MT[:, 0, :], start=True, stop=False),
                         r=[X_, MT], w=[PY])
                    kb.I('pe', lambda e: e.matmul(yo, lhsT=X_[:, 1, pr * 128:(pr + 1) * 128], rhs=MT[:, 1, :], start=False, stop=False),
                         r=[X_, MT], w=[PY])
                    kb.I('pe', lambda e: e.matmul(yo, lhsT=stb[:, 2 * pr:2 * pr + 2, :].rearrange("p a b -> p (a b)"), rhs=CS[:],
                                                  start=False, stop=False), r=[stb, CS], w=[PY])
                    for sc in range(2):
                        kb.I('pe', lambda e: e.matmul(PY[:, sc * 128:(sc + 1) * 128],
                                                      lhsT=XS[:, sc, pr * 128:(pr + 1) * 128], rhs=DI[:, hd, :],
                                                      start=False, stop=(sc == 1)), r=[XS, DI], w=[PY])
                    if DDBG < 4:
                        continue
                    PST = pst[0]
                    npst += 1
                    for sc in range(2):
                        kb.I('pe', lambda e: e.matmul(PST[:], lhsT=BTM[:, sc, g * 128:(g + 1) * 128], rhs=XD[:, sc, hd * 64:(hd + 1) * 64],
                                                      start=(sc == 0), stop=(sc == 1)), r=[BTM, XD], w=[PST])
                    kb.I('dve', lambda e: e.scalar_tensor_tensor(out=stf[:, hd, :], in0=stf[:, hd, :], scalar=ETOT[:, hd:hd + 1],
                                                                 in1=PST[:], op0=ALU.mult, op1=ALU.add), r=[stf, ETOT, PST], w=[stf])
                    kb.I('act', lambda e: e.copy(out=stb[:, hd, :], in_=stf[:, hd, :]), r=[stf], w=[stb])
                ci = g * 2 + hp
                if DDBG < 5:
                    continue
                for hh in range(2):
                    sl = slice(hh * 64, (hh + 1) * 64)
                    PYh = PYS[hh]
                    kb.I('dve', lambda e: e.tensor_tensor(out=YG[sl, hp, :], in0=PYh[sl, :], in1=SZ[sl, ci, :], op=ALU.mult),
                         r=[PYh, SZ], w=[YG])
                kb.I('act', lambda e: e.activation(out=SQ[:, hp, :], in_=YG[:, hp, :], func=AF.Square), r=[YG], w=[SQ])
            if DDBG < 6:
                continue
            SQH, SQL = sqh_r[g % 2], sql_r[g % 2]
            kb.I('dve', lambda e: e.tensor_copy(out=SQH[:], in_=SQ[:]), r=[SQ], w=[SQH])
            kb.I('dve', lambda e: e.tensor_tensor(out=SQL[:], in0=SQ[:], in1=SQH[:], op=ALU.subtract), r=[SQ, SQH], w=[SQL])
            ii = 0
            for hp in range(2):
                for SQP in (SQH, SQL):
                    kb.I('pe', lambda e: e.matmul(pss[:], lhsT=ones_bf[:], rhs=SQP[:, hp, :], start=(ii == 0), stop=(ii == 3)),
                         r=[ones_bf, SQP], w=[pss])
                    ii += 1
            kb.I('act', lambda e: e.activation(out=rn_t[:], in_=pss[:], func=AF.Sqrt, scale=1.0 / 256, bias=EPS), r=[pss], w=[rn_t])
            kb.I('dve', lambda e: e.reciprocal(out=ri_t[:], in_=rn_t[:]), r=[rn_t], w=[ri_t])
            for hp in range(2):
                ci = g * 2 + hp
                kb.I('dve',
                     lambda e: e.scalar_tensor_tensor(out=YN[:, ci, :], in0=YG[:, hp, :], scalar=GN[:, ci:ci + 1], in1=ri_t[:],
                                                      op0=ALU.mult, op1=ALU.mult), r=[YG, GN, ri_t], w=[YN])
        for hf in range(2 if (DDBG >= 9 and DPRE >= 9) else 0):
            kb.I('sp', lambda e: e.dma_start(out=ynT[hf * 1024:(hf + 1) * 1024, t0:t0 + 256].rearrange("(c p) t -> p c t", p=128),
                                             in_=YN[:, hf * 8:(hf + 1) * 8, :]), r=[YN], dma=YN)
    kb.end()
    if upto == 6:
        kb.root.close()
        return nc, kb

    kb.begin()
    watt = kb.sb([128, 8, 1024], BF16)
    wssm = kb.sb([128, 16, 1024], BF16)
    wout = kb.sb([128, 8, 1024], BF16)
    GP = kb.sb([128, 1024], F32)
    kb.I('sp', lambda e: e.dma_start(out=GP[:], in_=bass.AP(tensor=gpost.tensor, offset=0, ap=[[0, 128], [1, 1024]])), w=[GP], dma=GP)
    wst3 = [kb.sb([128, 1024], F32) for _ in range(2)]
    nw = 0
    for wsrc, wdst, nchunk in ((w_att, watt, 8), (w_ssm, wssm, 16), (w_out, wout, 8)):
        for c in range(nchunk):
            WS = wst3[nw % 2]
            kb.I('sp', lambda e: e.dma_start(out=WS[:], in_=wsrc[c * 128:(c + 1) * 128, :]), w=[WS], dma=WS)
            if nw % 2 == 0:
                kb.I('dve', lambda e: e.tensor_copy(out=wdst[:, c, :], in_=WS[:]), r=[WS], w=[wdst])
            else:
                kb.I('act', lambda e: e.copy(out=wdst[:, c, :], in_=WS[:]), r=[WS], w=[wdst])
            nw += 1
    TE = 256
    og_r = [kb.sb([128, 8, TE], BF16) for _ in range(2)]
    yn_r2 = [kb.sb([128, 16, TE], BF16) for _ in range(2)]
    ga_r = [kb.sb([128, 8, TE], BF16) for _ in range(2)]
    gs_r = [kb.sb([128, 8, TE], BF16) for _ in range(2)]
    x_r = [kb.sb([128, 2, 1024], F32) for _ in range(2)]
    o_r = [kb.sb([128, 2, 1024], F32) for _ in range(2)]
    mx_r = [kb.sb([128, 8, TE], BF16) for _ in range(2)]
    m1_r = [kb.sb([128, TE], F32) for _ in range(2)]
    m2_r = [kb.sb([128, TE], F32) for _ in range(2)]
    pa_r = [kb.psum([128, TE], F32) for _ in range(2)]
    pS_r = [kb.psum([128, TE], F32) for _ in range(2)]
    po_r = [kb.psum([128, 2, 512], F32) for _ in range(2)]
    sq2 = kb.sb([128, 512], F32)
    ss2 = [kb.sb([128, 2], F32) for _ in range(2)]
    s1 = [kb.sb([128, 1], F32) for _ in range(2)]
    s2 = [kb.sb([128, 1], F32) for _ in range(2)]
    s3 = [kb.sb([128, 1], F32) for _ in range(2)]
    t1_r = [kb.sb([128, 1024], F32) for _ in range(2)]
    nm = 0
    no = 0
    for te in range(S // TE):
        t0 = te * TE
        k2 = te % 2
        OGt, YNt, GAt, GSt, Xt, Ot, MX = og_r[k2], yn_r2[k2], ga_r[k2], gs_r[k2], x_r[k2], o_r[k2], mx_r[k2]
        eset = [OGt, YNt, GAt, GSt, Xt]
        kb.share(eset)
        kb.I('sp', lambda e: e.dma_start(out=OGt[:], in_=ogT[:, t0:t0 + TE].rearrange("(c p) t -> p c t", p=128)), w=[OGt], dma=OGt)
        for hf in range(2):
            kb.I('sp', lambda e: e.dma_start(out=YNt[:, hf * 8:(hf + 1) * 8, :],
                                             in_=ynT[hf * 1024:(hf + 1) * 1024, t0:t0 + TE].rearrange("(c p) t -> p c t", p=128)), w=[YNt], dma=YNt)
        kb.I('sp', lambda e: e.dma_start(out=GAt[:], in_=gaT[:, t0:t0 + TE].rearrange("(c p) t -> p c t", p=128)), w=[GAt], dma=GAt)
        kb.I('sp', lambda e: e.dma_start(out=GSt[:], in_=gsT[:, t0:t0 + TE].rearrange("(c p) t -> p c t", p=128)), w=[GSt], dma=GSt)
        kb.I('sp', lambda e: e.dma_start(out=Xt[:], in_=x_ap[t0:t0 + TE, :].rearrange("(j p) d -> p j d", p=128)), w=[Xt], dma=Xt)
        kb.seal(eset)
        for cb in range(8):
            PA, PSS = pa_r[nm % 2], pS_r[nm % 2]
            M1, M2 = m1_r[nm % 2], m2_r[nm % 2]
            nm += 1
            for c in range(8):
                kb.I('pe', lambda e: e.matmul(PA[:], lhsT=watt[:, c, cb * 128:(cb + 1) * 128], rhs=OGt[:, c, :],
                                              start=(c == 0), stop=(c == 7)), r=[watt, OGt], w=[PA])
            for c in range(16):
                kb.I('pe', lambda e: e.matmul(PSS[:], lhsT=wssm[:, c, cb * 128:(cb + 1) * 128], rhs=YNt[:, c, :],
                                              start=(c == 0), stop=(c == 15)), r=[wssm, YNt], w=[PSS])
            kb.I('dve', lambda e: e.tensor_tensor(out=M1[:], in0=PA[:], in1=GAt[:, cb, :], op=ALU.mult), r=[PA, GAt], w=[M1])
            kb.I('dve', lambda e: e.tensor_tensor(out=M2[:], in0=PSS[:], in1=GSt[:, cb, :], op=ALU.mult), r=[PSS, GSt], w=[M2])
            kb.I('dve', lambda e: e.tensor_tensor(out=MX[:, cb, :], in0=M1[:], in1=M2[:], op=ALU.add), r=[M1, M2], w=[MX])
        for ts in range(TE // 128):
            PO = po_r[no % 2]
            SS2, S1, S2, S3, T1 = ss2[no % 2], s1[no % 2], s2[no % 2], s3[no % 2], t1_r[no % 2]
            no += 1
            for half in range(2):
                for c in range(8):
                    kb.I('pe', lambda e: e.matmul(PO[:, half, :], lhsT=MX[:, c, ts * 128:(ts + 1) * 128],
                                                  rhs=wout[:, c, half * 512:(half + 1) * 512], start=(c == 0), stop=(c == 7)),
                         r=[MX, wout], w=[PO])
            for half in range(2):
                kb.I('act', lambda e: e.activation(out=sq2[:], in_=PO[:, half, :], func=AF.Square, accum_out=SS2[:, half:half + 1]),
                     r=[PO], w=[sq2, SS2])
            kb.I('dve', lambda e: e.tensor_tensor(out=S1[:], in0=SS2[:, 0:1], in1=SS2[:, 1:2], op=ALU.add), r=[SS2], w=[S1])
            kb.I('act', lambda e: e.activation(out=S2[:], in_=S1[:], func=AF.Sqrt, scale=1.0 / D, bias=EPS), r=[S1], w=[S2])
            kb.I('dve', lambda e: e.reciprocal(out=S3[:], in_=S2[:]), r=[S2], w=[S3])
            kb.I('dve', lambda e: e.scalar_tensor_tensor(out=T1[:], in0=PO[:].rearrange("p a b -> p (a b)"), scalar=S3[:, 0:1],
                                                         in1=GP[:], op0=ALU.mult, op1=ALU.mult), r=[PO, S3, GP], w=[T1])
            kb.I('dve', lambda e: e.tensor_tensor(out=Ot[:, ts, :], in0=T1[:], in1=Xt[:, ts, :], op=ALU.add), r=[T1, Xt], w=[Ot])
        kb.I('sp', lambda e: e.dma_start(out=y_ap[t0:t0 + TE, :].rearrange("(j p) d -> p j d", p=128), in_=Ot[:]), r=[Ot], dma=Ot)
    kb.end()
    if upto == 7:
        kb.root.close()
        return nc, kb
    kb.root.close()
    return nc, kb


def make_consts():
    c = np.zeros((128, C_END), np.float32)
    k = np.arange(128)[:, None]
    c[:, C_ID:C_ID + 128] = np.eye(128, dtype=np.float32)
    c[:, C_ONE:C_ONE + 128] = 1.0
    l = np.arange(256)[None, :]
    c[:, C_TRI0:C_TRI0 + 256] = (k <= l)
    c[:, C_TRI1:C_TRI1 + 256] = (k + 128 <= l)
    s = np.arange(128)[None, :]
    c[:, C_SUT:C_SUT + 128] = (k > s)
    c[:, C_MK0:C_MK0 + 256] = np.where(k <= l, 0.0, NEG)
    c[:, C_MK1:C_MK1 + 256] = np.where(k + 128 <= l, 0.0, NEG)
    return c


_CACHE = {}


def make_in_maps(inputs, B):
    f = lambda a: np.ascontiguousarray(np.asarray(a, dtype=np.float32))
    shared = {
        "w_in": f(inputs["w_in"][0]),
        "cst": make_consts(),
        "gpre_l": f(np.asarray(inputs["g_pre"][0]).reshape(8, 128).T),
        "rbT": f(np.asarray(inputs["rel_bias"]).T),
        "lamv": f(np.stack([inputs["att_lambda_q1"][0], inputs["att_lambda_k1"][0],
                            inputs["att_lambda_q2"][0], inputs["att_lambda_k2"][0]])),
        "subln_l": f(np.asarray(inputs["att_subln_g"][0]).reshape(128, 1)),
        "convw_l": f(np.asarray(inputs["conv_w"][0]).reshape(4, 32, 128).transpose(2, 1, 0)),
        "convb_l": f(np.asarray(inputs["conv_b"][0]).reshape(32, 128).T),
        "dtb": f(np.asarray(inputs["dt_bias"][0]).reshape(1, 32)),
        "alog": f(np.asarray(inputs["a_log"][0]).reshape(1, 32)),
        "dsk": f(np.asarray(inputs["d_skip"][0]).reshape(1, 32)),
        "gn_l": f(np.asarray(inputs["ssm_norm_g"][0]).reshape(16, 128).T),
        "w_att": f(inputs["w_att_proj"][0]),
        "w_ssm": f(inputs["w_ssm_proj"][0]),
        "w_out": f(inputs["w_out"][0]),
        "gpost": f(np.asarray(inputs["g_post"][0]).reshape(1, 1024)),
    }
    x = np.asarray(inputs["x"], dtype=np.float32)
    return [dict(shared, x=np.ascontiguousarray(x[b])) for b in range(B)]


def kernel(**inputs):
    x = np.asarray(inputs["x"])
    B, S, _ = x.shape
    assert B == 8
    if S not in _CACHE:
        _CACHE[S] = build_program(S)[0]
    nc = _CACHE[S]
    in_maps = make_in_maps(inputs, B)
    res = run_bass_kernel_spmd(nc, in_maps, core_ids=list(range(B)))
    return np.stack([np.asarray(r["y"], dtype=np.float32) for r in res.results], axis=0)
```
